# Optimizing a Trainium2 kernel written in Bass

```python
import math
import jax, jax.numpy as jnp
from jax import lax
import numpy as np

D_MODEL = 1024
BATCH = 4
SEQ = 4096
DEPTH = 2

HEAD_DIM = 64
N_HEADS_TOTAL = D_MODEL // HEAD_DIM
N_SB_HEADS = N_HEADS_TOTAL // 4
DIL_GROUPS = ((128, 1), (512, 4), (2048, 16))
N_DIL_GROUPS = len(DIL_GROUPS)
HEADS_PER_GROUP = (N_HEADS_TOTAL - N_SB_HEADS) // N_DIL_GROUPS
N_DIL_HEADS = HEADS_PER_GROUP * N_DIL_GROUPS
D_DIL = N_DIL_HEADS * HEAD_DIM
D_DIL_OUT = HEADS_PER_GROUP * HEAD_DIM
D_SB = N_SB_HEADS * HEAD_DIM
D_IN = 3 * D_DIL + 3 * D_SB + 2 * D_MODEL
D_FF = 128 * ((8 * D_MODEL // 3 + 127) // 128)
ROPE_THETA = 500000.0
ROPE_DIM = HEAD_DIM // 4
Q_BLOCK = 128
RMS_EPS = 1e-6

kernel_name = "hybrid_dilated_stickbreaking_macaron"


def rms_norm(x, gain):
    xf = x.astype(jnp.float32)
    xf = xf * lax.rsqrt(jnp.mean(xf * xf, axis=-1, keepdims=True) + RMS_EPS)
    return (xf * gain.astype(jnp.float32)).astype(x.dtype)


def swiglu(x, w_gate, w_up, w_down):
    return (jax.nn.silu(x @ w_gate) * (x @ w_up)) @ w_down


def rope_tables(seq_len):
    pos = jnp.arange(seq_len, dtype=jnp.float32)
    inv_freq = ROPE_THETA ** (-jnp.arange(0, ROPE_DIM, 2, dtype=jnp.float32) / ROPE_DIM)
    ang = pos[:, None] * inv_freq[None, :]
    return jnp.cos(ang), jnp.sin(ang)


def apply_partial_rope(x, cos, sin):
    half = ROPE_DIM // 2
    x1 = x[..., :half]
    x2 = x[..., half:ROPE_DIM]
    c = cos.astype(x.dtype)
    s = sin.astype(x.dtype)
    return jnp.concatenate([x1 * c - x2 * s, x2 * c + x1 * s, x[..., ROPE_DIM:]], axis=-1)


def dilated_window_attention(q, k, v, window, dilation):
    B, H, T, dh = q.shape
    span = window // dilation
    unit = span * dilation
    t_pad = -(-T // unit) * unit
    n_sub = t_pad // dilation
    n_blk = n_sub // span

    def to_blocks(a):
        a = jnp.pad(a, ((0, 0), (0, 0), (0, t_pad - T), (0, 0)))
        a = a.reshape(B, H, n_sub, dilation, dh).transpose(0, 1, 3, 2, 4)
        return a.reshape(B, H, dilation, n_blk, span, dh)

    qb, kb, vb = to_blocks(q), to_blocks(k), to_blocks(v)

    def with_prev(a):
        prev = jnp.pad(a, ((0, 0), (0, 0), (0, 0), (1, 0), (0, 0), (0, 0)))[:, :, :, :-1]
        return jnp.concatenate([prev, a], axis=4)

    kw, vw = with_prev(kb), with_prev(vb)
    s = jnp.einsum('bhrnqd,bhrnkd->bhrnqk', qb, kw).astype(jnp.float32) * (dh ** -0.5)
    qi = jnp.arange(span)[:, None]
    kj = jnp.arange(2 * span)[None, :]
    dist = qi + span - kj
    band = (dist >= 0) & (dist <= span)
    blk = jnp.arange(n_blk)[:, None, None]
    valid = band[None] & ((blk > 0) | (kj >= span)[None])
    s = jnp.where(valid, s, -jnp.inf)
    m = jnp.max(s, axis=-1, keepdims=True)
    p = jnp.exp(s - m)
    denom = jnp.sum(p, axis=-1, keepdims=True)
    o = jnp.einsum('bhrnqk,bhrnkd->bhrnqd', p, vw.astype(jnp.float32)) / denom
    lse = (m + jnp.log(denom))[..., 0]
    o = o.reshape(B, H, dilation, n_sub, dh).transpose(0, 1, 3, 2, 4).reshape(B, H, t_pad, dh)[:, :, :T]
    lse = lse.reshape(B, H, dilation, n_sub).transpose(0, 1, 3, 2).reshape(B, H, t_pad)[:, :, :T]
    return o, lse


def stick_breaking_attention(q, k, v):
    B, H, T, dh = q.shape
    n_blk = T // Q_BLOCK
    qb = q.reshape(B, H, n_blk, Q_BLOCK, dh).transpose(2, 0, 1, 3, 4)
    kpos = jnp.arange(T)
    vf = v.astype(jnp.float32)

    def block(args):
        q_blk, b = args
        z = jnp.einsum('bhqd,bhkd->bhqk', q_blk, k).astype(jnp.float32) * (dh ** -0.5)
        qpos = b * Q_BLOCK + jnp.arange(Q_BLOCK)
        past = kpos[None, :] < qpos[:, None]
        log_beta = jax.nn.log_sigmoid(z)
        log_keep = jnp.where(past, jax.nn.log_sigmoid(-z), 0.0)
        after = lax.cumsum(log_keep, axis=3, reverse=True) - log_keep
        w = jnp.where(past, jnp.exp(log_beta + after), 0.0)
        return jnp.einsum('bhqk,bhkd->bhqd', w, vf)

    o = lax.map(block, (qb, jnp.arange(n_blk)))
    return o.transpose(1, 2, 0, 3, 4).reshape(B, H, T, dh).astype(q.dtype)


def hybrid_mixer(h, w_in, w_proj_dil, w_proj_sb, w_out, cos, sin):
    B, T, _ = h.shape
    proj = h @ w_in
    o1 = 3 * D_DIL
    o2 = o1 + 3 * D_SB
    o3 = o2 + D_MODEL
    qkv_d = proj[..., :o1].reshape(B, T, 3, N_DIL_HEADS, HEAD_DIM).transpose(2, 0, 3, 1, 4)
    q_d = apply_partial_rope(qkv_d[0], cos, sin)
    k_d = apply_partial_rope(qkv_d[1], cos, sin)
    v_d = qkv_d[2]
    outs, lses = [], []
    for g, (window, dilation) in enumerate(DIL_GROUPS):
        hs = slice(g * HEADS_PER_GROUP, (g + 1) * HEADS_PER_GROUP)
        o, lse = dilated_window_attention(q_d[:, hs], k_d[:, hs], v_d[:, hs], window, dilation)
        outs.append(o)
        lses.append(lse)
    w_grp = jax.nn.softmax(jnp.stack(lses, axis=0), axis=0)
    o_dil = jnp.sum(w_grp[..., None] * jnp.stack(outs, axis=0), axis=0).astype(h.dtype)
    o_dil = o_dil.transpose(0, 2, 1, 3).reshape(B, T, D_DIL_OUT)
    qkv_s = proj[..., o1:o2].reshape(B, T, 3, N_SB_HEADS, HEAD_DIM).transpose(2, 0, 3, 1, 4)
    o_sb = stick_breaking_attention(qkv_s[0], qkv_s[1], qkv_s[2])
    o_sb = o_sb.transpose(0, 2, 1, 3).reshape(B, T, D_SB)
    gate_dil = jax.nn.sigmoid(proj[..., o2:o3])
    gate_sb = jax.nn.sigmoid(proj[..., o3:])
    y = gate_dil * (o_dil @ w_proj_dil) + gate_sb * (o_sb @ w_proj_sb)
    return y @ w_out


def setup_inputs(seed: int = 0) -> dict:
    key = jax.random.key(seed)
    ks = jax.random.split(key, 16)
    f32 = jnp.float32

    def dense(k, shape, fan_in):
        return jax.random.normal(k, shape, f32) * (fan_in ** -0.5)

    def gain(k, shape):
        return 1.0 + 0.05 * jax.random.normal(k, shape, f32)

    return {
        "x": jax.random.normal(ks[0], (BATCH, SEQ, D_MODEL), f32),
        "norm_ffn1": gain(ks[1], (DEPTH, D_MODEL)),
        "ffn1_w_gate": dense(ks[2], (DEPTH, D_MODEL, D_FF), D_MODEL),
        "ffn1_w_up": dense(ks[3], (DEPTH, D_MODEL, D_FF), D_MODEL),
        "ffn1_w_down": dense(ks[4], (DEPTH, D_FF, D_MODEL), D_FF),
        "norm_mix": gain(ks[5], (DEPTH, D_MODEL)),
        "w_in": dense(ks[6], (DEPTH, D_MODEL, D_IN), D_MODEL),
        "w_proj_dil": dense(ks[7], (DEPTH, D_DIL_OUT, D_MODEL), D_DIL_OUT),
        "w_proj_sb": dense(ks[8], (DEPTH, D_SB, D_MODEL), D_SB),
        "w_out": dense(ks[9], (DEPTH, D_MODEL, D_MODEL), D_MODEL),
        "norm_ffn2": gain(ks[10], (DEPTH, D_MODEL)),
        "ffn2_w_gate": dense(ks[11], (DEPTH, D_MODEL, D_FF), D_MODEL),
        "ffn2_w_up": dense(ks[12], (DEPTH, D_MODEL, D_FF), D_MODEL),
        "ffn2_w_down": dense(ks[13], (DEPTH, D_FF, D_MODEL), D_FF),
        "norm_final": gain(ks[14], (D_MODEL,)),
    }


def reference(x, norm_ffn1, ffn1_w_gate, ffn1_w_up, ffn1_w_down, norm_mix, w_in,
              w_proj_dil, w_proj_sb, w_out, norm_ffn2, ffn2_w_gate, ffn2_w_up,
              ffn2_w_down, norm_final):
    T = x.shape[1]
    cos, sin = rope_tables(T)
    for l in range(DEPTH):
        x = x + 0.5 * swiglu(rms_norm(x, norm_ffn1[l]), ffn1_w_gate[l], ffn1_w_up[l], ffn1_w_down[l])
        x = x + hybrid_mixer(rms_norm(x, norm_mix[l]), w_in[l], w_proj_dil[l], w_proj_sb[l],
                             w_out[l], cos, sin)
        x = x + 0.5 * swiglu(rms_norm(x, norm_ffn2[l]), ffn2_w_gate[l], ffn2_w_up[l], ffn2_w_down[l])
    return rms_norm(x, norm_final)
```

```python
from contextlib import ExitStack
import numpy as np
import ml_dtypes
import concourse.bass as bass
import concourse.mybir as mybir
from concourse.bass_utils import run_bass_kernel_spmd

F32 = mybir.dt.float32
BF16 = mybir.dt.bfloat16
AF = mybir.ActivationFunctionType
ALU = mybir.AluOpType
NPBF = ml_dtypes.bfloat16

D = 1024
KC = 8
DFF = 2816
TOWN = 2048
NT = 4
TS = 512
SEQ = 4096
EPS = 1e-6
ROPE_THETA = 500000.0
NCORES = 8


class TK:
    def __init__(self, nc, es):
        self.nc = nc
        self.es = es
        self.eng = {"pe": nc.tensor, "act": nc.scalar, "dve": nc.vector, "pool": nc.gpsimd, "sp": nc.sync}
        self.sem = {n: es.enter_context(nc.semaphore(n + "_sem")) for n in self.eng}
        self.cnt = {n: 0 for n in self.eng}
        self.waited = {n: {} for n in self.eng}
        self.lastw = {}
        self.readers = {}
        self.dsem = {}
        self.nps = 0

    def _deps(self, reads, writes):
        deps = []
        for k in list(reads) + list(writes):
            t = self.lastw.get(k)
            if t is not None:
                deps.append(t)
        for k in writes:
            deps.extend(self.readers.get(k, ()))
        return deps

    def _record(self, tok, reads, writes):
        for k in reads:
            self.readers.setdefault(k, []).append(tok)
        for k in writes:
            self.lastw[k] = tok
            self.readers[k] = []

    def _wait(self, en, deps):
        need = {}
        for t in deps:
            if t[0] == "e":
                _, src, n = t
                if src == en and en == "pe":
                    continue
                need[("e", src)] = max(need.get(("e", src), 0), n)
            else:
                _, key, n = t
                need[("d", key)] = max(need.get(("d", key), 0), self.dsem[key][1])
        e = self.eng[en]
        for src, n in need.items():
            if self.waited[en].get(src, 0) >= n:
                continue
            if src[0] == "e":
                e.wait_ge(self.sem[src[1]], n)
            else:
                e.wait_ge(self.dsem[src[1]][0], 16 * n)
            self.waited[en][src] = n

    def op(self, en, fn, reads=(), writes=()):
        self._wait(en, self._deps(reads, writes))
        ins = fn(self.eng[en])
        ins.then_inc(self.sem[en], 1)
        self.cnt[en] += 1
        tok = ("e", en, self.cnt[en])
        self._record(tok, reads, writes)
        return tok

    def dma(self, en, out, in_, reads=(), writes=(), key=None):
        self._wait(en, self._deps(reads, writes))
        if key not in self.dsem:
            self.dsem[key] = [self.es.enter_context(self.nc.semaphore("d%d_sem" % len(self.dsem))), 0]
        self.eng[en].dma_start(out=out, in_=in_).then_inc(self.dsem[key][0], 16)
        self.dsem[key][1] += 1
        tok = ("d", key, self.dsem[key][1])
        self._record(tok, reads, writes)
        return tok

    def finish(self):
        sp = self.eng["sp"]
        for key, (sem, n) in self.dsem.items():
            if n > 0:
                sp.wait_ge(sem, 16 * n)


class Ctx:
    def __init__(self, nc, es):
        self.nc = nc
        self.es = es
        self.tk = TK(nc, es)
        self.ps = [es.enter_context(nc.psum_tensor("psb%d" % i, [128, 512], F32)) for i in range(8)]
        self.pools = {"main": [list(range(8)), 0]}
        self.nslab = 0
        self.slabs = []
        self.slabi = 0

    def sb(self, name, shape, dt):
        return self.es.enter_context(self.nc.sbuf_tensor(name, shape, dt))

    def set_pools(self, d):
        self.pools = {k: [list(v), 0] for k, v in d.items()}

    def bank(self, pool="main"):
        p = self.pools[pool]
        i = p[0][p[1] % len(p[0])]
        p[1] += 1
        return self.ps[i], ("ps", i)

    def make_slabs(self, n):
        self.slabs = [self.sb("slab%d" % i, [128, 4096], BF16) for i in range(n)]

    def slab(self):
        i = self.slabi
        self.slabi = (self.slabi + 1) % len(self.slabs)
        return self.slabs[i], ("slab", i)

    def load_consts(self, d):
        out = {}
        for name, (ap, shape, dt) in d.items():
            t = self.sb("c_" + name, shape, dt)
            self.tk.dma("sp", t[:], ap, reads=(), writes=(("c", name),), key="consts")
            out[name] = t
        return out


def load_wslab_cols(cx, w, c0, ncols):
    slab, skey = cx.slab()
    view = slab[:, 0:8 * ncols].rearrange("p (k c) -> p k c", k=8)
    src = w[:, c0:c0 + ncols].rearrange("(k p) c -> p k c", p=128)
    cx.tk.dma("pool", view, src, reads=(), writes=(skey,), key=skey)
    return view, skey


def load_wslab_rows(cx, w, r0, nchunks):
    slab, skey = cx.slab()
    view = slab[:, 0:nchunks * 1024].rearrange("p (k c) -> p k c", k=nchunks)
    src = w[r0:r0 + 128 * nchunks, :].rearrange("(k p) c -> p k c", p=128)
    cx.tk.dma("pool", view, src, reads=(), writes=(skey,), key=skey)
    return view, skey


def emit_norm(cx, xT, hT, gain, ones_bf, sq, lnv, rstd, tag):
    tk = cx.tk
    for t in range(NT):
        ts = slice(t * TS, (t + 1) * TS)
        ps, pk = cx.bank()
        for c in range(KC):
            s = sq[c % 2]
            sk = ("sq", c % 2)
            tk.op("act", lambda e, s=s, c=c: e.activation(out=s[:], in_=xT[:, c, ts], func=AF.Square),
                  reads=(("x", c, t),), writes=(sk,))
            tk.op("pe", lambda e, s=s, c=c: e.matmul(ps[:], ones_bf[:], s[:], start=(c == 0), stop=(c == KC - 1)),
                  reads=(sk, ("c", "ones")), writes=(pk,))
        tk.op("act", lambda e: e.activation(out=lnv[:], in_=ps[:], func=AF.Ln, scale=1.0 / D, bias=EPS),
              reads=(pk,), writes=("lnv",))
        tk.op("act", lambda e: e.activation(out=rstd[:], in_=lnv[:], func=AF.Exp, scale=-0.5),
              reads=("lnv",), writes=("rstd",))
        for c in range(KC):
            tk.op("dve", lambda e, c=c: e.scalar_tensor_tensor(out=hT[:, c, ts], in0=xT[:, c, ts],
                                                                 scalar=gain[:, c:c + 1], in1=rstd[:],
                                                                 op0=ALU.mult, op1=ALU.mult),
                  reads=(("x", c, t), "rstd", ("c", tag)), writes=(("h", c, t),))


def emit_ffn(cx, xT, hT, act, sg, wg, wu, wd):
    tk = cx.tk
    groups = [(0, 4), (4, 4), (8, 4), (12, 4), (16, 4), (20, 2)]
    for (f0, nf) in groups:
        gsl, gk = load_wslab_cols(cx, wg, f0 * 128, nf * 128)
        usl, uk = load_wslab_cols(cx, wu, f0 * 128, nf * 128)
        dsl, dk = load_wslab_rows(cx, wd, f0 * 128, nf)
        for fc in range(nf):
            for t in range(NT):
                ts = slice(t * TS, (t + 1) * TS)
                pg, pgk = cx.bank()
                pu, puk = cx.bank()
                for c in range(KC):
                    tk.op("pe", lambda e, c=c: e.matmul(pg[:], gsl[:, c, fc * 128:(fc + 1) * 128], hT[:, c, ts],
                                                        start=(c == 0), stop=(c == KC - 1)),
                          reads=(gk, ("h", c, t)), writes=(pgk,))
                for c in range(KC):
                    tk.op("pe", lambda e, c=c: e.matmul(pu[:], usl[:, c, fc * 128:(fc + 1) * 128], hT[:, c, ts],
                                                        start=(c == 0), stop=(c == KC - 1)),
                          reads=(uk, ("h", c, t)), writes=(puk,))
                s = sg[(fc * NT + t) % 2]
                sk = ("sg", (fc * NT + t) % 2)
                tk.op("act", lambda e, s=s: e.activation(out=s[:], in_=pg[:], func=AF.Silu),
                      reads=(pgk,), writes=(sk,))
                tk.op("dve", lambda e, s=s: e.tensor_tensor(out=act[:, fc, ts], in0=pu[:], in1=s[:], op=ALU.mult),
                      reads=(puk, sk), writes=(("act", fc, t),))
        for dc in range(KC):
            for t in range(NT):
                ts = slice(t * TS, (t + 1) * TS)
                pd, pdk = cx.bank()
                for fc in range(nf):
                    tk.op("pe", lambda e, fc=fc: e.matmul(pd[:], dsl[:, fc, dc * 128:(dc + 1) * 128], act[:, fc, ts],
                                                          start=(fc == 0), stop=(fc == nf - 1)),
                          reads=(dk, ("act", fc, t)), writes=(pdk,))
                tk.op("dve", lambda e: e.scalar_tensor_tensor(out=xT[:, dc, ts], in0=pd[:], scalar=0.5,
                                                              in1=xT[:, dc, ts], op0=ALU.mult, op1=ALU.add),
                      reads=(pdk, ("x", dc, t)), writes=(("x", dc, t),))


def load_x(cx, xT, x_in):
    for c in range(KC):
        for t in range(NT):
            cx.tk.dma("sp", xT[:, c, t * TS:(t + 1) * TS], x_in[c * 128:(c + 1) * 128, t * TS:(t + 1) * TS],
                      reads=(), writes=(("x", c, t),), key="xin")


def store_x(cx, xT, x_out, key):
    for c in range(KC):
        cx.tk.dma("sp", x_out[c * 128:(c + 1) * 128, :], xT[:, c, :],
                  reads=tuple(("x", c, t) for t in range(NT)), writes=(("xo", key, c),), key=key)


def load_gain(cx, name, ap):
    t = cx.sb("g_" + name, [128, KC], F32)
    cx.tk.dma("sp", t[:], ap, reads=(), writes=(("c", name),), key="consts")
    return t


def build_A():
    nc = bass.Bass("TRN2", target_bir_lowering=False)
    dt = lambda n, s, d=F32, k="ExternalInput": nc.dram_tensor(n, s, d, kind=k).ap()
    x_in = dt("xT", [D, TOWN])
    g1 = dt("g1", [128, KC]); gm = dt("gm", [128, KC])
    wg = dt("wg", [D, DFF]); wu = dt("wu", [D, DFF]); wd = dt("wd", [DFF, D])
    wq = dt("wq", [D, 1024]); wk = dt("wk", [D, 1024]); wv = dt("wv", [D, 1024])
    wqs = dt("wqs", [D, 768]); wks = dt("wks", [D, 768])
    ropeC = dt("ropeC", [128, TOWN]); ropeS = dt("ropeS", [128, TOWN])
    ones_in = dt("ones", [128, 128], BF16)
    x_out = dt("xT_out", [D, TOWN], F32, "ExternalOutput")
    q_out = dt("qT_out", [1024, TOWN], BF16, "ExternalOutput")
    k_out = dt("kT_out", [1024, TOWN], BF16, "ExternalOutput")
    v_out = dt("v_out", [TOWN, 1024], BF16, "ExternalOutput")
    with ExitStack() as es:
        cx = Ctx(nc, es)
        tk = cx.tk
        xT = cx.sb("xT_sb", [128, KC, TOWN], F32)
        hT = cx.sb("hT_sb", [128, KC, TOWN], BF16)
        act = cx.sb("act_sb", [128, 4, TOWN], BF16)
        sg = [cx.sb("sg%d" % i, [128, TS], F32) for i in range(2)]
        sq = [cx.sb("sq%d" % i, [128, TS], BF16) for i in range(2)]
        lnv = cx.sb("lnv", [128, TS], F32)
        rstd = cx.sb("rstd", [128, TS], F32)
        cx.make_slabs(5)
        stage = [cx.sb("stage%d" % i, [128, TOWN], BF16) for i in range(2)]
        vst = [cx.sb("vst%d" % i, [128, 512], BF16) for i in range(2)]
        t1 = [cx.sb("t1_%d" % i, [128, TS], F32) for i in range(2)]
        t2 = [cx.sb("t2_%d" % i, [128, TS], F32) for i in range(2)]
        es.enter_context(nc.Block())
        cs = cx.load_consts({"ones": (ones_in, [128, 128], BF16),
                             "ropeC": (ropeC, [128, TOWN], F32), "ropeS": (ropeS, [128, TOWN], F32)})
        g1s = load_gain(cx, "g1", g1)
        gms = load_gain(cx, "gm", gm)
        load_x(cx, xT, x_in)
        emit_norm(cx, xT, hT, g1s, cs["ones"], sq, lnv, rstd, "g1")
        emit_ffn(cx, xT, hT, act, sg, wg, wu, wd)
        store_x(cx, xT, x_out, "xout")
        emit_norm(cx, xT, hT, gms, cs["ones"], sq, lnv, rstd, "gm")
        si = 0
        ti = 0
        for (w, wsw, out) in ((wq, wqs, q_out), (wk, wks, k_out)):
            for u in range(4):
                sl, sk = load_wslab_cols(cx, w, u * 256, 256)
                rot = u < 3
                if rot:
                    ssl, ssk = load_wslab_cols(cx, wsw, u * 256, 256)
                for ch in range(2):
                    st = stage[si % 2]
                    stk = ("stage", si % 2)
                    si += 1
                    for t in range(NT):
                        ts = slice(t * TS, (t + 1) * TS)
                        p1, p1k = cx.bank()
                        for c in range(KC):
                            tk.op("pe", lambda e, c=c: e.matmul(p1[:], sl[:, c, ch * 128:(ch + 1) * 128], hT[:, c, ts],
                                                                start=(c == 0), stop=(c == KC - 1)),
                                  reads=(sk, ("h", c, t)), writes=(p1k,))
                        if rot:
                            p2, p2k = cx.bank()
                            for c in range(KC):
                                tk.op("pe", lambda e, c=c: e.matmul(p2[:], ssl[:, c, ch * 128:(ch + 1) * 128],
                                                                    hT[:, c, ts], start=(c == 0), stop=(c == KC - 1)),
                                      reads=(ssk, ("h", c, t)), writes=(p2k,))
                            a1 = t1[ti % 2]; a2 = t2[ti % 2]
                            k1 = ("t1", ti % 2); k2 = ("t2", ti % 2)
                            ti += 1
                            tk.op("dve", lambda e: e.tensor_tensor(out=a1[:], in0=p1[:], in1=cs["ropeC"][:, ts],
                                                                   op=ALU.mult),
                                  reads=(p1k, ("c", "ropeC")), writes=(k1,))
                            tk.op("dve", lambda e: e.tensor_tensor(out=a2[:], in0=p2[:], in1=cs["ropeS"][:, ts],
                                                                   op=ALU.mult),
                                  reads=(p2k, ("c", "ropeS")), writes=(k2,))
                            tk.op("pool", lambda e: e.tensor_tensor(out=st[:, ts], in0=a1[:], in1=a2[:], op=ALU.add),
                                  reads=(k1, k2), writes=(stk,))
                        else:
                            tk.op("act", lambda e: e.activation(out=st[:, ts], in_=p1[:], func=AF.Copy),
                                  reads=(p1k,), writes=(stk,))
                    r0 = u * 256 + ch * 128
                    tk.dma("sp", out[r0:r0 + 128, :], st[:], reads=(stk,), writes=(("qk_out", r0, id(out)),),
                           key=("stq", (si - 1) % 2))
        vi = 0
        for vs_ in range(2):
            sl, sk = load_wslab_cols(cx, wv, vs_ * 512, 512)
            for tb in range(TOWN // 128):
                t = tb // 4
                p1, p1k = cx.bank()
                for c in range(KC):
                    tk.op("pe", lambda e, c=c: e.matmul(p1[:], hT[:, c, tb * 128:(tb + 1) * 128], sl[:, c, :],
                                                        start=(c == 0), stop=(c == KC - 1)),
                          reads=(sk, ("h", c, t)), writes=(p1k,))
                st = vst[vi % 2]; stk = ("vst", vi % 2)
                vi += 1
                if vi % 2:
                    tk.op("act", lambda e: e.activation(out=st[:], in_=p1[:], func=AF.Copy), reads=(p1k,), writes=(stk,))
                else:
                    tk.op("dve", lambda e: e.tensor_copy(out=st[:], in_=p1[:]), reads=(p1k,), writes=(stk,))
                tk.dma("sp", v_out[tb * 128:(tb + 1) * 128, vs_ * 512:(vs_ + 1) * 512], st[:], reads=(stk,),
                       writes=(("v_out", tb, vs_),), key=("stv", (vi - 1) % 2))
        tk.finish()
    return nc


DILS = (1, 4, 16)


def build_B():
    nc = bass.Bass("TRN2", target_bir_lowering=False)
    dt = lambda n, s, d=BF16, k="ExternalInput": nc.dram_tensor(n, s, d, kind=k).ap()
    qd = dt("qd", [3, 128, SEQ]); kd = dt("kd", [3, 128, SEQ]); vd = dt("vd", [3, SEQ, 128])
    qs = dt("qs", [128, SEQ]); ks = dt("ks", [128, SEQ]); vs = dt("vs", [SEQ, 128])
    ones_in = dt("ones", [128, 128]); tri_in = dt("tri", [128, 128])
    mdil_in = dt("mdil", [128, 256]); mst_in = dt("mst", [128, 4, 512])
    od_out = dt("od_out", [128, SEQ], BF16, "ExternalOutput")
    os_out = dt("os_out", [128, SEQ], BF16, "ExternalOutput")
    with ExitStack() as es:
        cx = Ctx(nc, es)
        tk = cx.tk
        Q = [cx.sb("Q%d" % i, [128, SEQ], BF16) for i in range(2)]
        K = [cx.sb("K%d" % i, [128, SEQ], BF16) for i in range(2)]
        V = [cx.sb("V%d" % i, [128, 32, 128], BF16) for i in range(2)]
        NK = cx.sb("NK", [128, SEQ], BF16)
        NUM = cx.sb("NUM", [64, SEQ], F32)
        DEN = cx.sb("DEN", [64, SEQ], F32)
        ost = [cx.sb("ost%d" % i, [64, SEQ], BF16) for i in range(2)]
        Eb = [cx.sb("Eb%d" % i, [128, 256], BF16) for i in range(3)]
        Pb = [cx.sb("Pb%d" % i, [128, 256], BF16) for i in range(3)]
        Ee = [cx.sb("Ee%d" % i, [128, 512], F32) for i in range(3)]
        NL = [cx.sb("NL%d" % i, [128, 512], BF16) for i in range(3)]
        NLm = [cx.sb("NLm%d" % i, [128, 512], BF16) for i in range(3)]
        Wb = [cx.sb("Wb%d" % i, [128, 512], BF16) for i in range(3)]
        Wm = [cx.sb("Wm%d" % i, [128, 512], BF16) for i in range(3)]
        Sacc = [cx.sb("Sacc%d" % i, [128, 512], BF16) for i in range(2)]
        es.enter_context(nc.Block())
        cs = cx.load_consts({"ones": (ones_in, [128, 128], BF16), "tri": (tri_in, [128, 128], BF16),
                             "mdil": (mdil_in, [128, 256], BF16), "mst": (mst_in, [128, 4, 512], BF16)})
        ones = cs["ones"]; tri = cs["tri"]; mdil = cs["mdil"]; mst = cs["mst"]

        def load_group(g):
            b = g % 2
            d = DILS[g]
            tk.dma("sp", Q[b][:], qd[g], reads=(), writes=(("Q", b),), key=("Q", b))
            tk.dma("sp", K[b][:], kd[g], reads=(), writes=(("K", b),), key=("K", b))
            nb = 32 // d
            vv = V[b][:].rearrange("p (r m) c -> p r m c", r=d)
            src = vd[g].rearrange("(m i r) c -> i r m c", i=128, r=d)
            for r in range(d):
                tk.dma("sp", vv[:, r], src[:, r], reads=(), writes=(("V", b),), key=("V", b))

        cnt = 0
        cx.set_pools({"s": [0, 1, 2], "acc": [3, 4, 5, 6]})
        for jj in range(2):
            for g in range(3):
                load_group(g)
                b = g % 2
                d = DILS[g]
                nb = 32 // d
                pr = slice(64 * jj, 64 * jj + 64)
                Qv = Q[b][pr, :].rearrange("p (n r) -> p r n", r=d)
                Kv = K[b][pr, :].rearrange("p (n r) -> p r n", r=d)
                Vv = V[b][:].rearrange("p (r m) c -> p r m c", r=d)
                NUMv = NUM[:].rearrange("p (n r) -> p r n", r=d)
                DENv = DEN[:].rearrange("p (n r) -> p r n", r=d)
                per = min(4, nb)
                for r in range(d):
                    for a in range(nb // per):
                        pn, pnk = cx.bank("acc")
                        pdn, pdk = cx.bank("acc")
                        m_lo = max(a * per - 1, 0)
                        m_hi = (a + 1) * per - 1
                        started = set()
                        for m in range(m_lo, m_hi + 1):
                            nq = 2 if m + 1 < nb else 1
                            psb, psk = cx.bank("s")
                            i3 = cnt % 3
                            cnt += 1
                            rhs = Qv[:, r, m * 128:(m + nq) * 128]
                            tk.op("pe", lambda e: e.matmul(psb[:, 0:128 * nq], Kv[:, r, m * 128:(m + 1) * 128], rhs, start=True, stop=True),
                                  reads=(("Q", b), ("K", b)), writes=(psk,))
                            tk.op("act", lambda e: e.activation(out=Eb[i3][:, 0:128 * nq], in_=psb[:, 0:128 * nq],
                                                                func=AF.Exp, scale=0.125),
                                  reads=(psk,), writes=(("Eb", i3),))
                            tk.op("dve", lambda e: e.tensor_tensor(out=Pb[i3][:, 0:128 * nq], in0=Eb[i3][:, 0:128 * nq],
                                                                   in1=mdil[:, 0:128 * nq], op=ALU.mult),
                                  reads=(("Eb", i3), ("c", "mdil")), writes=(("Pb", i3),))
                            for h in range(nq):
                                n = m + h
                                if n < a * per or n >= (a + 1) * per:
                                    continue
                                first = n not in started
                                started.add(n)
                                last = (h == 0)
                                cs_ = slice((n - a * per) * 128, (n - a * per + 1) * 128)
                                tk.op("pe", lambda e: e.matmul(pn[0:64, cs_], Vv[:, r, m, 64 * jj:64 * jj + 64],
                                                               Pb[i3][:, h * 128:(h + 1) * 128], start=first, stop=last),
                                      reads=(("Pb", i3), ("V", b)), writes=(pnk,))
                                tk.op("pe", lambda e: e.matmul(pdn[0:64, cs_], ones[:, 0:64],
                                                               Pb[i3][:, h * 128:(h + 1) * 128], start=first, stop=last),
                                      reads=(("Pb", i3), ("c", "ones")), writes=(pdk,))
                        dstn = NUMv[:, r, a * per * 128:(a + 1) * per * 128]
                        dstd = DENv[:, r, a * per * 128:(a + 1) * per * 128]
                        srcn = pn[0:64, 0:per * 128]
                        srcd = pdn[0:64, 0:per * 128]
                        if g == 0:
                            tk.op("act", lambda e: e.activation(out=dstn, in_=srcn, func=AF.Copy),
                                  reads=(pnk,), writes=("NUM",))
                            tk.op("act", lambda e: e.activation(out=dstd, in_=srcd, func=AF.Copy),
                                  reads=(pdk,), writes=("DEN",))
                        else:
                            tk.op("dve", lambda e: e.tensor_tensor(out=dstn, in0=srcn, in1=dstn, op=ALU.add),
                                  reads=(pnk, "NUM"), writes=("NUM",))
                            tk.op("dve", lambda e: e.tensor_tensor(out=dstd, in0=srcd, in1=dstd, op=ALU.add),
                                  reads=(pdk, "DEN"), writes=("DEN",))
            for t in range(SEQ // 512):
                ts = slice(t * 512, (t + 1) * 512)
                tk.op("dve", lambda e: e.reciprocal(out=DEN[:, ts], in_=DEN[:, ts]), reads=("DEN",), writes=("DEN",))
                tk.op("dve", lambda e: e.tensor_tensor(out=ost[jj][:, ts], in0=NUM[:, ts], in1=DEN[:, ts], op=ALU.mult),
                      reads=("NUM", "DEN"), writes=(("ost", jj),))
            tk.dma("sp", od_out[64 * jj:64 * jj + 64, :], ost[jj][:], reads=(("ost", jj),),
                   writes=(("od_out", jj),), key=("ost", jj))

        tk.dma("sp", Q[1][:], qs, reads=(), writes=(("Q", 1),), key=("Q", 1))
        tk.dma("sp", K[1][:], ks, reads=(), writes=(("K", 1),), key=("K", 1))
        tk.dma("sp", V[1][:], vs.rearrange("(m p) c -> p m c", p=128), reads=(), writes=(("V", 1),), key=("V", 1))
        for t in range(SEQ // 512):
            ts = slice(t * 512, (t + 1) * 512)
            tk.op("act", lambda e: e.mul(out=NK[:, ts], in_=K[1][:, ts], mul=-0.125), reads=(("K", 1),), writes=("NK",))
        Qs, Ks, Vs = Q[1], K[1], V[1]
        cx.set_pools({"z": [0, 1, 2], "a": [3, 4, 5], "o": [6, 7]})
        for jj in range(2):
            pr = slice(64 * jj, 64 * jj + 64)
            steps = []
            for c in range(SEQ // 512):
                for kb in range(4 * c + 3, -1, -1):
                    steps.append((c, kb))
            st = {}
            osb = ost[jj]
            pO = None

            def stage1(i):
                c, kb = steps[i]
                i3 = i % 3
                ts = slice(c * 512, (c + 1) * 512)
                pz, pzk = cx.bank("z")
                tk.op("pe", lambda e: e.matmul(pz[:], Ks[pr, kb * 128:(kb + 1) * 128], Qs[pr, ts], start=True, stop=True),
                      reads=(("Q", 1), ("K", 1)), writes=(pzk,))
                tk.op("act", lambda e: e.activation(out=Ee[i3][:], in_=pz[:], func=AF.Exp, scale=0.125),
                      reads=(pzk,), writes=(("Ee", i3),))
                tk.op("act", lambda e: e.activation(out=NL[i3][:], in_=Ee[i3][:], func=AF.Ln, bias=1.0),
                      reads=(("Ee", i3),), writes=(("NL", i3),))
                u = kb - 4 * c
                if u >= 0:
                    tk.op("dve", lambda e: e.tensor_tensor(out=NLm[i3][:], in0=NL[i3][:], in1=mst[:, u, :], op=ALU.mult),
                          reads=(("NL", i3), ("c", "mst")), writes=(("NLm", i3),))
                    st[i] = (NLm[i3], ("NLm", i3))
                else:
                    st[i] = (NL[i3], ("NL", i3))

            def stage2(i):
                c, kb = steps[i]
                i3 = i % 3
                ts = slice(c * 512, (c + 1) * 512)
                nl, nlk = st[i]
                first = (kb == 4 * c + 3)
                pa, pak = cx.bank("a")
                sa = Sacc[jj]
                sak = ("Sacc", jj)
                tk.op("pe", lambda e: e.matmul(pa[:], tri[:], nl[:], start=True, stop=False),
                      reads=(nlk, ("c", "tri")), writes=(pak,))
                tk.op("pe", lambda e: e.matmul(pa[:], NK[pr, kb * 128:(kb + 1) * 128], Qs[pr, ts], start=False, stop=first),
                      reads=("NK", ("Q", 1)), writes=(pak,))
                if not first:
                    tk.op("pe", lambda e: e.matmul(pa[:], ones[:], sa[:], start=False, stop=True),
                          reads=(sak, ("c", "ones")), writes=(pak,))
                    tk.op("pool", lambda e: e.tensor_tensor(out=sa[:], in0=sa[:], in1=nl[:], op=ALU.add),
                          reads=(sak, nlk), writes=(sak,))
                else:
                    tk.op("pool", lambda e: e.tensor_copy(out=sa[:], in_=nl[:]), reads=(nlk,), writes=(sak,))
                tk.op("act", lambda e: e.activation(out=Wb[i3][:], in_=pa[:], func=AF.Exp, scale=-1.0),
                      reads=(pak,), writes=(("Wb", i3),))
                u = kb - 4 * c
                if u >= 0:
                    tk.op("dve", lambda e: e.tensor_tensor(out=Wm[i3][:], in0=Wb[i3][:], in1=mst[:, u, :], op=ALU.mult),
                          reads=(("Wb", i3), ("c", "mst")), writes=(("Wm", i3),))
                    st[i] = (Wm[i3], ("Wm", i3))
                else:
                    st[i] = (Wb[i3], ("Wb", i3))

            def stage3(i):
                nonlocal pO
                c, kb = steps[i]
                ts = slice(c * 512, (c + 1) * 512)
                w, wk_ = st.pop(i)
                first = (kb == 4 * c + 3)
                if first:
                    pO = cx.bank("o")
                po, pok = pO
                tk.op("pe", lambda e: e.matmul(po[0:64, :], Vs[:, kb, 64 * jj:64 * jj + 64], w[:], start=first, stop=(kb == 0)),
                      reads=(wk_, ("V", 1)), writes=(pok,))
                if kb == 0:
                    tk.op("act", lambda e: e.activation(out=osb[:, ts], in_=po[0:64, :], func=AF.Copy),
                          reads=(pok,), writes=(("ost", jj),))

            n = len(steps)
            for i in range(n + 2):
                if i < n:
                    stage1(i)
                if 0 <= i - 1 < n:
                    stage2(i - 1)
                if 0 <= i - 2 < n:
                    stage3(i - 2)
            tk.dma("sp", os_out[64 * jj:64 * jj + 64, :], osb[:], reads=(("ost", jj),),
                   writes=(("os_out", jj),), key=("ost", jj))
        tk.finish()
    return nc


def build_C():
    nc = bass.Bass("TRN2", target_bir_lowering=False)
    dt = lambda n, s, d=F32, k="ExternalInput": nc.dram_tensor(n, s, d, kind=k).ap()
    x_in = dt("xT", [D, TOWN])
    od_in = dt("odT", [256, TOWN], BF16); os_in = dt("osT", [256, TOWN], BF16)
    gm = dt("gm", [128, KC]); g2 = dt("g2", [128, KC]); gf = dt("gf", [128, KC])
    wgd = dt("wgd", [D, D]); wgs = dt("wgs", [D, D])
    wpd = dt("wpd", [256, D]); wps = dt("wps", [256, D]); wo = dt("wo", [D, D])
    wg = dt("wg", [D, DFF]); wu = dt("wu", [D, DFF]); wd = dt("wd", [DFF, D])
    ones_in = dt("ones", [128, 128], BF16)
    x_out = dt("xT_out", [D, TOWN], F32, "ExternalOutput")
    xn_out = dt("xnT_out", [D, TOWN], F32, "ExternalOutput")
    with ExitStack() as es:
        cx = Ctx(nc, es)
        tk = cx.tk
        xT = cx.sb("xT_sb", [128, KC, TOWN], F32)
        hT = cx.sb("hT_sb", [128, KC, TOWN], BF16)
        yT = cx.sb("yT_sb", [128, KC, TOWN], BF16)
        act = yT[:, 0:4, :]
        oT = cx.sb("oT_sb", [128, 4, TOWN], BF16)
        pw = cx.sb("pw_sb", [128, 4, D], BF16)
        sg = [cx.sb("sg%d" % i, [128, TS], F32) for i in range(2)]
        sq = [cx.sb("sq%d" % i, [128, TS], BF16) for i in range(2)]
        gd = [cx.sb("gd%d" % i, [128, TS], F32) for i in range(2)]
        gs = [cx.sb("gs%d" % i, [128, TS], F32) for i in range(2)]
        lnv = cx.sb("lnv", [128, TS], F32)
        rstd = cx.sb("rstd", [128, TS], F32)
        cx.make_slabs(4)
        es.enter_context(nc.Block())
        cs = cx.load_consts({"ones": (ones_in, [128, 128], BF16)})
        gms = load_gain(cx, "gm", gm)
        g2s = load_gain(cx, "g2", g2)
        gfs = load_gain(cx, "gf", gf)
        load_x(cx, xT, x_in)
        for j in range(2):
            tk.dma("sp", oT[:, j, :], od_in[j * 128:(j + 1) * 128, :], reads=(), writes=(("o", j),), key="oin")
            tk.dma("sp", oT[:, 2 + j, :], os_in[j * 128:(j + 1) * 128, :], reads=(), writes=(("o", 2 + j),), key="oin")
        emit_norm(cx, xT, hT, gms, cs["ones"], sq, lnv, rstd, "gm")
        pdsl = pw[:, 0:2, :]
        pssl = pw[:, 2:4, :]
        pdk = ("pw", 0)
        psk = ("pw", 1)
        tk.dma("pool", pdsl, wpd.rearrange("(k p) c -> p k c", p=128), reads=(), writes=(pdk,), key="pw")
        tk.dma("pool", pssl, wps.rearrange("(k p) c -> p k c", p=128), reads=(), writes=(psk,), key="pw")
        gi = 0
        for u in range(4):
            gdsl, gdk = load_wslab_cols(cx, wgd, u * 256, 256)
            gssl, gsk = load_wslab_cols(cx, wgs, u * 256, 256)
            for ch in range(2):
                oc = u * 2 + ch
                for t in range(NT):
                    ts = slice(t * TS, (t + 1) * TS)
                    p1, p1k = cx.bank()
                    p2, p2k = cx.bank()
                    p3, p3k = cx.bank()
                    p4, p4k = cx.bank()
                    for c in range(KC):
                        tk.op("pe", lambda e, c=c: e.matmul(p1[:], gdsl[:, c, ch * 128:(ch + 1) * 128], hT[:, c, ts],
                                                            start=(c == 0), stop=(c == KC - 1)),
                              reads=(gdk, ("h", c, t)), writes=(p1k,))
                    for c in range(KC):
                        tk.op("pe", lambda e, c=c: e.matmul(p2[:], gssl[:, c, ch * 128:(ch + 1) * 128], hT[:, c, ts],
                                                            start=(c == 0), stop=(c == KC - 1)),
                              reads=(gsk, ("h", c, t)), writes=(p2k,))
                    for c in range(2):
                        tk.op("pe", lambda e, c=c: e.matmul(p3[:], pdsl[:, c, oc * 128:(oc + 1) * 128], oT[:, c, ts],
                                                            start=(c == 0), stop=(c == 1)),
                              reads=(pdk, ("o", c)), writes=(p3k,))
                    for c in range(2):
                        tk.op("pe", lambda e, c=c: e.matmul(p4[:], pssl[:, c, oc * 128:(oc + 1) * 128], oT[:, 2 + c, ts],
                                                            start=(c == 0), stop=(c == 1)),
                              reads=(psk, ("o", 2 + c)), writes=(p4k,))
                    a = gd[gi % 2]; b_ = gs[gi % 2]
                    ak = ("gd", gi % 2); bk = ("gs", gi % 2)
                    gi += 1
                    tk.op("act", lambda e: e.activation(out=a[:], in_=p1[:], func=AF.Sigmoid), reads=(p1k,), writes=(ak,))
                    tk.op("act", lambda e: e.activation(out=b_[:], in_=p2[:], func=AF.Sigmoid), reads=(p2k,), writes=(bk,))
                    tk.op("dve", lambda e: e.tensor_tensor(out=a[:], in0=p3[:], in1=a[:], op=ALU.mult),
                          reads=(p3k, ak), writes=(ak,))
                    tk.op("dve", lambda e: e.tensor_tensor(out=b_[:], in0=p4[:], in1=b_[:], op=ALU.mult),
                          reads=(p4k, bk), writes=(bk,))
                    tk.op("pool", lambda e: e.tensor_tensor(out=yT[:, oc, ts], in0=a[:], in1=b_[:], op=ALU.add),
                          reads=(ak, bk), writes=(("y", oc, t),))
        for u in range(2):
            wsl, wk_ = load_wslab_cols(cx, wo, u * 512, 512)
            for ch in range(4):
                oc = u * 4 + ch
                for t in range(NT):
                    ts = slice(t * TS, (t + 1) * TS)
                    p1, p1k = cx.bank()
                    for c in range(KC):
                        tk.op("pe", lambda e, c=c: e.matmul(p1[:], wsl[:, c, ch * 128:(ch + 1) * 128], yT[:, c, ts],
                                                            start=(c == 0), stop=(c == KC - 1)),
                              reads=(wk_, ("y", c, t)), writes=(p1k,))
                    tk.op("dve", lambda e: e.tensor_tensor(out=xT[:, oc, ts], in0=p1[:], in1=xT[:, oc, ts], op=ALU.add),
                          reads=(p1k, ("x", oc, t)), writes=(("x", oc, t),))
        emit_norm(cx, xT, hT, g2s, cs["ones"], sq, lnv, rstd, "g2")
        emit_ffn(cx, xT, hT, act, sg, wg, wu, wd)
        store_x(cx, xT, x_out, "xout")
        for t in range(NT):
            ts = slice(t * TS, (t + 1) * TS)
            ps, pk = cx.bank()
            for c in range(KC):
                s = sq[c % 2]; sk = ("sq", c % 2)
                tk.op("act", lambda e, s=s, c=c: e.activation(out=s[:], in_=xT[:, c, ts], func=AF.Square),
                      reads=(("x", c, t),), writes=(sk,))
                tk.op("pe", lambda e, s=s, c=c: e.matmul(ps[:], cs["ones"][:], s[:], start=(c == 0), stop=(c == KC - 1)),
                      reads=(sk, ("c", "ones")), writes=(pk,))
            tk.op("act", lambda e: e.activation(out=lnv[:], in_=ps[:], func=AF.Ln, scale=1.0 / D, bias=EPS),
                  reads=(pk,), writes=("lnv",))
            tk.op("act", lambda e: e.activation(out=rstd[:], in_=lnv[:], func=AF.Exp, scale=-0.5),
                  reads=("lnv",), writes=("rstd",))
            for c in range(KC):
                o = sg[c % 2]; ok = ("sg", c % 2)
                tk.op("dve", lambda e, c=c, o=o: e.scalar_tensor_tensor(out=o[:], in0=xT[:, c, ts], scalar=gfs[:, c:c + 1],
                                                                         in1=rstd[:], op0=ALU.mult, op1=ALU.mult),
                      reads=(("x", c, t), "rstd", ("c", "gf")), writes=(ok,))
                tk.dma("sp", xn_out[c * 128:(c + 1) * 128, ts], o[:], reads=(ok,), writes=(("xn", c, t),),
                       key=("sgo", c % 2))
        tk.finish()
    return nc


_PROGS = {}


def _prog(name):
    if name not in _PROGS:
        _PROGS[name] = {"A": build_A, "B": build_B, "C": build_C}[name]()
    return _PROGS[name]


def _gain_lay(g):
    return np.ascontiguousarray(g.reshape(KC, 128).T).astype(np.float32)


def _consts():
    ones = np.ones((128, 128), dtype=NPBF)
    j = np.arange(128)[:, None]
    s = np.arange(128)[None, :]
    tri = (j >= s).astype(np.float32).astype(NPBF)
    ik = np.arange(128)[:, None]
    iq = np.arange(128)[None, :]
    mdil = np.concatenate([(iq >= ik), (iq <= ik)], axis=1).astype(np.float32).astype(NPBF)
    mst = np.zeros((128, 4, 512), dtype=np.float32)
    tq = np.arange(512)[None, :]
    for u in range(4):
        mst[:, u, :] = ((128 * u + ik) < tq)
    return ones, tri, mdil, mst.astype(NPBF)


def _rope_tables(pos0):
    half = 8
    pos = np.arange(pos0, pos0 + TOWN, dtype=np.float32)
    inv_freq = (np.float32(ROPE_THETA) ** (-np.arange(0, 16, 2, dtype=np.float32) / np.float32(16))).astype(np.float32)
    ang = (pos[:, None] * inv_freq[None, :]).astype(np.float32)
    cos = np.cos(ang).astype(np.float32).T
    sin = np.sin(ang).astype(np.float32).T
    C = np.ones((128, TOWN), dtype=np.float32)
    S = np.zeros((128, TOWN), dtype=np.float32)
    for hh in range(2):
        b = 64 * hh
        C[b:b + 8] = cos
        C[b + 8:b + 16] = cos
        S[b:b + 8] = -sin
        S[b + 8:b + 16] = sin
    return C, S


def _swap_cols(w768):
    w = w768.reshape(D, 12, 64).copy()
    a = w[:, :, 0:8].copy()
    w[:, :, 0:8] = w[:, :, 8:16]
    w[:, :, 8:16] = a
    return np.ascontiguousarray(w.reshape(D, 768))


def kernel(x, norm_ffn1, ffn1_w_gate, ffn1_w_up, ffn1_w_down, norm_mix, w_in, w_proj_dil, w_proj_sb, w_out,
           norm_ffn2, ffn2_w_gate, ffn2_w_up, ffn2_w_down, norm_final):
    f32 = lambda a: np.ascontiguousarray(np.asarray(a, dtype=np.float32))
    x = f32(x)
    ones, tri, mdil, mst = _consts()
    cores = list(range(NCORES))
    xT = [np.ascontiguousarray(x[c // 2, (c % 2) * TOWN:(c % 2 + 1) * TOWN, :].T) for c in cores]
    ropes = [_rope_tables(s * TOWN) for s in range(2)]
    depth = norm_ffn1.shape[0]
    xn = None
    for l in range(depth):
        wi = f32(w_in[l])
        wq = np.ascontiguousarray(np.concatenate([wi[:, 0:768], wi[:, 2304:2560]], axis=1))
        wk = np.ascontiguousarray(np.concatenate([wi[:, 768:1536], wi[:, 2560:2816]], axis=1))
        wv = np.ascontiguousarray(np.concatenate([wi[:, 1536:2304], wi[:, 2816:3072]], axis=1))
        wqs = _swap_cols(wi[:, 0:768])
        wks = _swap_cols(wi[:, 768:1536])
        wgd = np.ascontiguousarray(wi[:, 3072:4096])
        wgs = np.ascontiguousarray(wi[:, 4096:5120])
        common_a = {"g1": _gain_lay(f32(norm_ffn1[l])), "gm": _gain_lay(f32(norm_mix[l])),
                    "wg": f32(ffn1_w_gate[l]), "wu": f32(ffn1_w_up[l]), "wd": f32(ffn1_w_down[l]),
                    "wq": wq, "wk": wk, "wv": wv, "wqs": wqs, "wks": wks, "ones": ones}
        in_a = [dict(common_a, xT=xT[c], ropeC=ropes[c % 2][0], ropeS=ropes[c % 2][1]) for c in cores]
        ra = run_bass_kernel_spmd(_prog("A"), in_a, core_ids=cores).results
        xT = [np.asarray(ra[c]["xT_out"]) for c in cores]
        in_b = []
        for c in cores:
            b, s = c // 2, c % 2
            qf = np.concatenate([np.asarray(ra[2 * b]["qT_out"]), np.asarray(ra[2 * b + 1]["qT_out"])], axis=1)
            kf = np.concatenate([np.asarray(ra[2 * b]["kT_out"]), np.asarray(ra[2 * b + 1]["kT_out"])], axis=1)
            vf = np.concatenate([np.asarray(ra[2 * b]["v_out"]), np.asarray(ra[2 * b + 1]["v_out"])], axis=0)
            rows = [slice(64 * (4 * g + 2 * s), 64 * (4 * g + 2 * s) + 128) for g in range(3)]
            srow = slice(768 + 128 * s, 768 + 128 * s + 128)
            in_b.append({
                "qd": np.ascontiguousarray(np.stack([qf[r] for r in rows])),
                "kd": np.ascontiguousarray(np.stack([kf[r] for r in rows])),
                "vd": np.ascontiguousarray(np.stack([vf[:, r] for r in rows])),
                "qs": np.ascontiguousarray(qf[srow]), "ks": np.ascontiguousarray(kf[srow]),
                "vs": np.ascontiguousarray(vf[:, srow]),
                "ones": ones, "tri": tri, "mdil": mdil, "mst": mst})
        rb = run_bass_kernel_spmd(_prog("B"), in_b, core_ids=cores).results
        in_c = []
        common_c = {"gm": _gain_lay(f32(norm_mix[l])), "g2": _gain_lay(f32(norm_ffn2[l])),
                    "gf": _gain_lay(f32(norm_final)), "wgd": wgd, "wgs": wgs,
                    "wpd": f32(w_proj_dil[l]), "wps": f32(w_proj_sb[l]), "wo": f32(w_out[l]),
                    "wg": f32(ffn2_w_gate[l]), "wu": f32(ffn2_w_up[l]), "wd": f32(ffn2_w_down[l]), "ones": ones}
        for c in cores:
            b, s = c // 2, c % 2
            tsl = slice(s * TOWN, (s + 1) * TOWN)
            od = np.concatenate([np.asarray(rb[2 * b]["od_out"])[:, tsl], np.asarray(rb[2 * b + 1]["od_out"])[:, tsl]], axis=0)
            os_ = np.concatenate([np.asarray(rb[2 * b]["os_out"])[:, tsl], np.asarray(rb[2 * b + 1]["os_out"])[:, tsl]], axis=0)
            in_c.append(dict(common_c, xT=xT[c], odT=np.ascontiguousarray(od), osT=np.ascontiguousarray(os_)))
        rc = run_bass_kernel_spmd(_prog("C"), in_c, core_ids=cores).results
        xT = [np.asarray(rc[c]["xT_out"]) for c in cores]
        xn = [np.asarray(rc[c]["xnT_out"]) for c in cores]
    out = np.empty((4, SEQ, D), dtype=np.float32)
    for c in cores:
        out[c // 2, (c % 2) * TOWN:(c % 2 + 1) * TOWN, :] = xn[c].T
    return out
```
